# Optimizing a Trainium2 kernel written in Bass

```python
import math
import jax
import jax.numpy as jnp
from jax import lax
import numpy as np

D_MODEL = 4096
BATCH = 2
SEQ = 8192
DEPTH = 2

D_MIX = D_MODEL
N_MIXERS = 4
D_GROUP = D_MIX // N_MIXERS

SC_WIDTH = 3
CF_WIDTH = 31
GDN_HEAD_DIM = 128
GDN_HEADS = D_GROUP // GDN_HEAD_DIM
GDN_CONV = 4
GDN_CHUNK = 64
POOL_WINDOWS = (2, 4, 8, 16)
POOL_GROUP = D_GROUP // len(POOL_WINDOWS)
D_FF = -(-8 * D_MODEL // (3 * 256)) * 256

NORM_EPS = 1e-6

IN_SIZES = (
    D_GROUP, D_GROUP, D_GROUP,
    2 * D_GROUP,
    D_GROUP, D_GROUP, D_GROUP, D_GROUP,
    GDN_HEADS, GDN_HEADS,
    D_GROUP,
)
IN_COLS = sum(IN_SIZES)

kernel_name = "hybrid_parallel_conv_deltanet_pool"


def rms_norm(x, w):
    xf = x.astype(jnp.float32)
    y = xf * lax.rsqrt(jnp.mean(xf * xf, axis=-1, keepdims=True) + NORM_EPS)
    return (y * w.astype(jnp.float32)).astype(x.dtype)


def layer_norm(x, w, b):
    xf = x.astype(jnp.float32)
    mu = jnp.mean(xf, axis=-1, keepdims=True)
    xc = xf - mu
    y = xc * lax.rsqrt(jnp.mean(xc * xc, axis=-1, keepdims=True) + NORM_EPS)
    return (y * w.astype(jnp.float32) + b.astype(jnp.float32)).astype(x.dtype)


def l2norm(x):
    xf = x.astype(jnp.float32)
    return xf * lax.rsqrt(jnp.sum(xf * xf, axis=-1, keepdims=True) + NORM_EPS)


def causal_dwconv(x, w):
    k, c = w.shape
    return lax.conv_general_dilated(
        x, w[:, None, :].astype(x.dtype), window_strides=(1,), padding=[(k - 1, 0)],
        dimension_numbers=("NWC", "WIO", "NWC"), feature_group_count=c)


def split_cols(h):
    outs, start = [], 0
    for size in IN_SIZES:
        outs.append(h[..., start:start + size])
        start += size
    return outs


def gated_delta_rule(q, k, v, g, beta):
    b, s, nh, dk = q.shape
    dv = v.shape[-1]
    c = GDN_CHUNK
    n = s // c

    def blocks(t):
        t = t.reshape((b, n, c, nh) + t.shape[3:])
        return jnp.moveaxis(t, 3, 1)

    q = blocks(q * (dk ** -0.5))
    k = blocks(k)
    v = blocks(v)
    g = blocks(g)
    beta = blocks(beta)
    gc = jnp.cumsum(g, axis=-1)
    causal = jnp.tril(jnp.ones((c, c), dtype=bool))
    strict = jnp.tril(jnp.ones((c, c), dtype=bool), k=-1)
    decay = jnp.exp(jnp.where(causal, gc[..., :, None] - gc[..., None, :], -jnp.inf))
    kb = k * beta[..., None]
    vb = v * beta[..., None]
    m = jnp.where(strict, jnp.einsum("bhnik,bhnjk->bhnij", kb, k) * decay, 0.0)
    eye = jnp.eye(c, dtype=q.dtype)
    t_inv = lax.linalg.triangular_solve(eye + m, jnp.broadcast_to(eye, m.shape),
                                        left_side=True, lower=True)
    u = jnp.einsum("bhnij,bhnjv->bhniv", t_inv, vb)
    w = jnp.einsum("bhnij,bhnjk->bhnik", t_inv, kb * jnp.exp(gc)[..., None])
    a_intra = jnp.einsum("bhnik,bhnjk->bhnij", q, k) * decay
    q_dec = q * jnp.exp(gc)[..., None]
    k_dec = k * jnp.exp(gc[..., -1:] - gc)[..., None]
    g_last = jnp.exp(gc[..., -1])

    def step(state, inp):
        q_i, k_i, u_i, w_i, a_i, gl_i = inp
        v_new = u_i - jnp.einsum("bhck,bhkv->bhcv", w_i, state)
        o_i = (jnp.einsum("bhck,bhkv->bhcv", q_i, state)
               + jnp.einsum("bhij,bhjv->bhiv", a_i, v_new))
        state = state * gl_i[..., None, None] + jnp.einsum("bhck,bhcv->bhkv", k_i, v_new)
        return state, o_i

    xs = tuple(jnp.moveaxis(t, 2, 0) for t in (q_dec, k_dec, u, w, a_intra, g_last))
    state0 = jnp.zeros((b, nh, dk, dv), q.dtype)
    _, o = lax.scan(step, state0, xs)
    o = jnp.transpose(o, (1, 0, 3, 2, 4))
    return o.reshape(b, s, nh, dv)


def multiscale_pool(u, pool_w, pool_scale):
    bsz, s, _ = u.shape
    uf = u.astype(jnp.float32).reshape(bsz, s, len(POOL_WINDOWS), POOL_GROUP)
    cs = jnp.cumsum(uf, axis=1)
    pos = jnp.arange(s)
    outs = []
    for gi, win in enumerate(POOL_WINDOWS):
        c = cs[:, :, gi]
        prev = jnp.pad(c, ((0, 0), (win, 0), (0, 0)))[:, :s]
        cnt = jnp.minimum(pos + 1, win).astype(jnp.float32)[None, :, None]
        outs.append((c - prev) / cnt - uf[:, :, gi])
    p = jnp.stack(outs, axis=2).astype(u.dtype)
    y = jnp.einsum("bsgc,gcd->bsgd", p, pool_w)
    return y.reshape(bsz, s, D_GROUP) * pool_scale


def setup_inputs(seed: int = 0) -> dict:
    key = jax.random.key(seed)
    ks = jax.random.split(key, 24)
    f32 = jnp.float32

    def nrm(k, shape, scale):
        return jax.random.normal(k, shape, f32) * scale

    def gain(k, shape):
        return 1.0 + 0.02 * jax.random.normal(k, shape, f32)

    dt = jnp.exp(jax.random.uniform(ks[10], (DEPTH, GDN_HEADS), f32,
                                    math.log(1e-3), math.log(1e-1)))
    return {
        "x": jax.random.normal(ks[0], (BATCH, SEQ, D_MODEL), f32),
        "attn_norm_w": gain(ks[1], (DEPTH, D_MODEL)),
        "w_in": nrm(ks[2], (DEPTH, D_MODEL, IN_COLS), D_MODEL ** -0.5),
        "sc_conv_w": nrm(ks[3], (DEPTH, SC_WIDTH, D_GROUP), SC_WIDTH ** -0.5),
        "cf_conv_w": nrm(ks[4], (DEPTH, CF_WIDTH, D_GROUP), CF_WIDTH ** -0.5),
        "cf_conv_b": nrm(ks[5], (DEPTH, D_GROUP), 0.02),
        "cf_ln_w": gain(ks[6], (DEPTH, D_GROUP)),
        "cf_ln_b": nrm(ks[7], (DEPTH, D_GROUP), 0.02),
        "gdn_conv_w": nrm(ks[8], (DEPTH, GDN_CONV, 3 * D_GROUP), GDN_CONV ** -0.5),
        "gdn_a_log": jnp.log(jax.random.uniform(ks[9], (DEPTH, GDN_HEADS), f32, 1.0, 16.0)),
        "gdn_dt_bias": dt + jnp.log(-jnp.expm1(-dt)),
        "gdn_norm_w": gain(ks[11], (DEPTH, GDN_HEAD_DIM)),
        "pool_w": nrm(ks[12], (DEPTH, len(POOL_WINDOWS), POOL_GROUP, POOL_GROUP), POOL_GROUP ** -0.5),
        "pool_scale": gain(ks[13], (DEPTH, D_GROUP)),
        "w_out": nrm(ks[14], (DEPTH, D_MIX, D_MODEL), D_MIX ** -0.5),
        "ffn_norm_w": gain(ks[15], (DEPTH, D_MODEL)),
        "w_gate": nrm(ks[16], (DEPTH, D_MODEL, D_FF), D_MODEL ** -0.5),
        "w_up": nrm(ks[17], (DEPTH, D_MODEL, D_FF), D_MODEL ** -0.5),
        "w_down": nrm(ks[18], (DEPTH, D_FF, D_MODEL), D_FF ** -0.5),
        "final_norm_w": gain(ks[19], (D_MODEL,)),
    }


def reference(x, attn_norm_w, w_in, sc_conv_w, cf_conv_w, cf_conv_b, cf_ln_w, cf_ln_b,
              gdn_conv_w, gdn_a_log, gdn_dt_bias, gdn_norm_w, pool_w, pool_scale,
              w_out, ffn_norm_w, w_gate, w_up, w_down, final_norm_w):
    bsz, s, _ = x.shape
    f32 = jnp.float32
    hs = (bsz, s, GDN_HEADS, GDN_HEAD_DIM)
    for i in range(DEPTH):
        h = rms_norm(x, attn_norm_w[i])
        proj = h @ w_in[i]
        (sc_b, sc_c, sc_h, cf_in, g_q, g_k, g_v, g_z, g_a, g_b, pool_u) = split_cols(proj)

        y_a = sc_b * causal_dwconv(sc_c * sc_h, sc_conv_w[i])

        glu = cf_in[..., :D_GROUP] * jax.nn.sigmoid(cf_in[..., D_GROUP:])
        cf = causal_dwconv(glu, cf_conv_w[i]) + cf_conv_b[i]
        y_b = jax.nn.silu(layer_norm(cf, cf_ln_w[i], cf_ln_b[i]))

        qkv = jax.nn.silu(causal_dwconv(jnp.concatenate([g_q, g_k, g_v], axis=-1), gdn_conv_w[i]))
        q = l2norm(qkv[..., :D_GROUP].reshape(hs))
        k = l2norm(qkv[..., D_GROUP:2 * D_GROUP].reshape(hs))
        v = qkv[..., 2 * D_GROUP:].reshape(hs).astype(f32)
        beta = jax.nn.sigmoid(g_b.astype(f32))
        gdec = -jnp.exp(gdn_a_log[i].astype(f32)) * jax.nn.softplus(
            g_a.astype(f32) + gdn_dt_bias[i].astype(f32))
        o = gated_delta_rule(q, k, v, gdec, beta)
        o = rms_norm(o, gdn_norm_w[i]) * jax.nn.silu(g_z.reshape(hs).astype(f32))
        y_c = o.reshape(bsz, s, D_GROUP).astype(x.dtype)

        y_d = multiscale_pool(pool_u, pool_w[i], pool_scale[i])

        mix = jnp.concatenate([y_a, y_b, y_c, y_d], axis=-1)
        x = x + mix @ w_out[i]

        h = rms_norm(x, ffn_norm_w[i])
        x = x + (jax.nn.silu(h @ w_gate[i]) * (h @ w_up[i])) @ w_down[i]
    return rms_norm(x, final_norm_w)
```

```python
import numpy as np
import concourse.bass as bass
import concourse.mybir as mybir

F32 = mybir.dt.float32
BF16 = mybir.dt.bfloat16
AF = mybir.ActivationFunctionType
ALU = mybir.AluOpType
AX = mybir.AxisListType

ENGS = ("pe", "act", "dve", "pool", "sp")
N_DMA_SEMS = 24


class Buf:
    __slots__ = ("name", "w", "r")

    def __init__(self, name=""):
        self.name = name
        self.w = None
        self.r = {}


class Prog:
    def __init__(self, nc):
        self.nc = nc
        self.q = {e: [] for e in ENGS}
        self.sem = {e: nc.alloc_semaphore(name=f"s_{e}") for e in ENGS}
        self.cnt = {e: 0 for e in ENGS}
        self.waited = {e: {} for e in ENGS}
        self.dsem = [nc.alloc_semaphore(name=f"s_dma{i}") for i in range(N_DMA_SEMS)]
        self.dcnt = [0] * N_DMA_SEMS
        self.drr = 0
        self.semobj = {}
        for e in ENGS:
            self.semobj[("e", e)] = self.sem[e]
        for i in range(N_DMA_SEMS):
            self.semobj[("d", i)] = self.dsem[i]
        self.n_inst = 0

    def _need(self, eng, tok):
        if tok is None:
            return
        key, val = tok
        if self.waited[eng].get(key, 0) >= val:
            return
        self.waited[eng][key] = val
        sem = self.semobj[key]
        self.q[eng].append(lambda E, sem=sem, val=val: E.wait_ge(sem, val))
        self.n_inst += 1

    def _deps(self, eng, reads, writes):
        for b in reads:
            self._need(eng, b.w)
        for b in writes:
            self._need(eng, b.w)
            for k, v in b.r.items():
                self._need(eng, (k, v))

    def _mark(self, tok, reads, writes):
        k, v = tok
        for b in reads:
            if b.r.get(k, 0) < v:
                b.r[k] = v
        for b in writes:
            b.w = tok
            b.r = {}

    def op(self, eng, fn, reads=(), writes=()):
        self._deps(eng, reads, writes)
        self.cnt[eng] += 1
        v = self.cnt[eng]
        sem = self.sem[eng]
        self.q[eng].append(lambda E, fn=fn, sem=sem: fn(E).then_inc(sem, 1))
        self.n_inst += 1
        tok = (("e", eng), v)
        self._mark(tok, reads, writes)
        return tok

    def raw(self, eng, fn):
        self.q[eng].append(lambda E, fn=fn: fn(E))
        self.n_inst += 1

    def dma(self, eng, out, in_, reads=(), writes=(), **kw):
        i = self.drr
        self.drr = (i + 1) % N_DMA_SEMS
        self._deps(eng, reads, writes)
        if self.dcnt[i] > 0:
            self._need(eng, (("d", i), self.dcnt[i]))
        self.dcnt[i] += 16
        sem = self.dsem[i]
        self.q[eng].append(
            lambda E, out=out, in_=in_, sem=sem, kw=kw: E.dma_start(out=out, in_=in_, **kw).then_inc(sem, 16))
        self.n_inst += 1
        tok = (("d", i), self.dcnt[i])
        self._mark(tok, reads, writes)
        return tok

    def barrier(self):
        for e in ENGS:
            for f in ENGS:
                if f != e and self.cnt[f] > 0:
                    self._need(e, (("e", f), self.cnt[f]))
            for i in range(N_DMA_SEMS):
                if self.dcnt[i] > 0:
                    self._need(e, (("d", i), self.dcnt[i]))

    def finish(self, eng="sp"):
        for i in range(N_DMA_SEMS):
            if self.dcnt[i] > 0:
                self._need(eng, (("d", i), self.dcnt[i]))
        for e in ENGS:
            if e != eng and self.cnt[e] > 0:
                self._need(eng, (("e", e), self.cnt[e]))

    def emit(self):
        nc = self.nc
        q = self.q
        with nc.Block() as block:
            @block.tensor
            def _(E):
                for f in q["pe"]:
                    f(E)

            @block.scalar
            def _(E):
                for f in q["act"]:
                    f(E)

            @block.vector
            def _(E):
                for f in q["dve"]:
                    f(E)

            @block.gpsimd
            def _(E):
                for f in q["pool"]:
                    f(E)

            @block.sync
            def _(E):
                for f in q["sp"]:
                    f(E)

import numpy as np
import ml_dtypes
import concourse.bass as bass
import concourse.mybir as mybir
from concourse.bass_utils import run_bass_kernel_spmd

EPS = 1e-6
TT = 512
DG = 1024
NWB = 4


class Cfg:
    def __init__(self, D=4096, S=8192, FF=11008):
        self.D, self.S, self.FF = D, S, FF
        self.KD = D // 128
        self.KF = FF // 128
        self.NT = S // 4
        self.DMIX = 4 * DG
        self.KM = self.DMIX // 128


class Ctx:
    def __init__(self, nc, P, cfg):
        self.nc, self.P, self.cfg = nc, P, cfg
        self.ps = [nc.alloc_psum_tensor(f"ps{i}", [128, 512], F32) for i in range(8)]
        self.Bps = [Buf(f"ps{i}") for i in range(8)]
        self.psi = 0
        wk = max(cfg.KD, (cfg.KF + 1) // 2, cfg.KM)
        self.wk = wk
        self.wb = [nc.alloc_sbuf_tensor(f"wb{i}", [128, wk, 128], BF16) for i in range(NWB)]
        self.Bwb = [Buf(f"wb{i}") for i in range(NWB)]
        self.wbi = 0
        self.ones = nc.alloc_sbuf_tensor("ones", [128, 128], F32)
        self.Bones = Buf("ones")
        P.op("dve", lambda E: E.memset(self.ones[:], 1.0), writes=[self.Bones])
        self.ones16 = nc.alloc_sbuf_tensor("ones16", [128, 128], BF16)
        P.op("dve", lambda E: E.memset(self.ones16[:], 1.0), writes=[self.Bones])

    def psum(self):
        i = self.psi
        self.psi = (i + 1) % 8
        return self.ps[i], self.Bps[i]

    def wtile(self, src_ap, kc):
        i = self.wbi
        self.wbi = (i + 1) % NWB
        t, B = self.wb[i], self.Bwb[i]
        self.P.dma("pool", t[:, 0:kc, :], src_ap, writes=[B])
        return t, B


def mm_group(P, ps, Bps, pairs, reads):
    n = len(pairs)

    def fn(E):
        ins = None
        for k, (l, r) in enumerate(pairs):
            ins = E.matmul(ps, l, r, start=(k == 0), stop=(k == n - 1))
        return ins
    return P.op("pe", fn, reads=reads, writes=[Bps])


def emit_ffn_stage(cx, t, last):
    nc, P, cfg = cx.nc, cx.P, cx.cfg
    D, FF, KD, KF, KM, NT = cfg.D, cfg.FF, cfg.KD, cfg.KF, cfg.KM, cfg.NT
    A = nc.alloc_sbuf_tensor
    act = A("f_act", [128, max(KF, KM), TT], BF16)
    Bact = [Buf(f"act{m}") for m in range(max(KF, KM))]
    mix = act
    hT = A("f_hT", [128, KD, TT], BF16)
    BhT = [Buf(f"hT{m}") for m in range(KD)]
    vec = A("f_vec", [128, 2 * KD], F32)
    Bvec = Buf("vec")
    acc = A("f_acc", [128, TT], F32); Bacc = Buf("acc")
    rstd = A("f_rstd", [128, TT], F32); Brstd = Buf("rstd")
    NTMP = 3
    xt = [A(f"f_xt{i}", [128, TT], F32) for i in range(NTMP)]; Bxt = [Buf() for _ in range(NTMP)]
    x1 = [A(f"f_x1{i}", [128, TT], F32) for i in range(NTMP)]; Bx1 = [Buf() for _ in range(NTMP)]
    sq = [A(f"f_sq{i}", [128, TT], F32) for i in range(NTMP)]; Bsq = [Buf() for _ in range(NTMP)]
    gs = [A(f"f_gs{i}", [128, TT], F32) for i in range(NTMP)]; Bgs = [Buf() for _ in range(NTMP)]
    us = [A(f"f_us{i}", [128, TT], F32) for i in range(NTMP)]; Bus = [Buf() for _ in range(NTMP)]
    Bx1d = Buf("x1dram")
    Boutd = Buf("outdram")
    P.dma("sp", vec[:], t["vec"], writes=[Bvec])
    mixT = t["mixT"].rearrange("(k p) n -> p k n", p=128)
    wout, wg, wu, wd = t["wout"], t["wg"], t["wu"], t["wd"]
    kh = (KF + 1) // 2
    it = 0
    for tt in range(NT // TT):
        ts = slice(tt * TT, (tt + 1) * TT)
        P.dma("sp", mix[:, 0:KM, :], mixT[:, :, ts], writes=Bact[0:KM])
        for m in range(KD):
            i = it % NTMP; it += 1
            w, Bw = cx.wtile(wout[m], KM)
            ps, Bp = cx.psum()
            mm_group(P, ps[:], Bp, [(w[:, k, :], mix[:, k, :]) for k in range(KM)], reads=[Bw] + Bact[0:KM])
            P.dma("sp", xt[i][:], t["xT"][m * 128:(m + 1) * 128, ts], writes=[Bxt[i]])
            P.op("dve", lambda E, i=i, ps=ps: E.tensor_tensor(out=x1[i][:], in0=ps[:], in1=xt[i][:], op=ALU.add),
                 reads=[Bp, Bxt[i]], writes=[Bx1[i]])
            P.dma("sp", t["x1T"][m * 128:(m + 1) * 128, ts], x1[i][:], reads=[Bx1[i]], writes=[Bx1d])
            P.op("act", lambda E, i=i: E.activation(sq[i][:], x1[i][:], AF.Square), reads=[Bx1[i]], writes=[Bsq[i]])
            if m == 0:
                P.op("dve", lambda E, i=i: E.tensor_copy(acc[:], sq[i][:]), reads=[Bsq[i]], writes=[Bacc])
            else:
                P.op("dve", lambda E, i=i: E.tensor_tensor(out=acc[:], in0=acc[:], in1=sq[i][:], op=ALU.add),
                     reads=[Bsq[i], Bacc], writes=[Bacc])
            P.op("act", lambda E, i=i, m=m: E.activation(hT[:, m, :], x1[i][:], AF.Copy, scale=vec[:, m:m + 1]),
                 reads=[Bx1[i], Bvec], writes=[BhT[m]])
        ps, Bp = cx.psum()
        mm_group(P, ps[:], Bp, [(cx.ones[:], acc[:])], reads=[cx.Bones, Bacc])
        P.op("act", lambda E, ps=ps: E.activation(rstd[:], ps[:], AF.Sqrt, bias=EPS, scale=1.0 / D), reads=[Bp], writes=[Brstd])
        P.op("dve", lambda E: E.reciprocal(rstd[:], rstd[:]), reads=[Brstd], writes=[Brstd])
        for m in range(KF):
            i = it % NTMP; it += 1
            w1, Bw1 = cx.wtile(wg[m], KD)
            w2, Bw2 = cx.wtile(wu[m], KD)
            pg, Bpg = cx.psum()
            mm_group(P, pg[:], Bpg, [(w1[:, k, :], hT[:, k, :]) for k in range(KD)], reads=[Bw1] + BhT)
            pu, Bpu = cx.psum()
            mm_group(P, pu[:], Bpu, [(w2[:, k, :], hT[:, k, :]) for k in range(KD)], reads=[Bw2] + BhT)
            P.op("dve", lambda E, i=i, pg=pg: E.tensor_tensor(out=gs[i][:], in0=pg[:], in1=rstd[:], op=ALU.mult),
                 reads=[Bpg, Brstd], writes=[Bgs[i]])
            P.op("dve", lambda E, i=i, pu=pu: E.tensor_tensor(out=us[i][:], in0=pu[:], in1=rstd[:], op=ALU.mult),
                 reads=[Bpu, Brstd], writes=[Bus[i]])
            P.op("act", lambda E, i=i: E.activation(sq[i][:], gs[i][:], AF.Silu), reads=[Bgs[i]], writes=[Bsq[i]])
            P.op("dve", lambda E, i=i, m=m: E.tensor_tensor(out=act[:, m, :], in0=sq[i][:], in1=us[i][:], op=ALU.mult),
                 reads=[Bsq[i], Bus[i]], writes=[Bact[m]])
        for m in range(D // 128):
            i = it % NTMP; it += 1
            wa, Bwa = cx.wtile(wd[m][:, 0:kh, :], kh)
            wb, Bwb = cx.wtile(wd[m][:, kh:KF, :], KF - kh)
            ps, Bp = cx.psum()
            pairs = [(wa[:, k, :], act[:, k, :]) for k in range(kh)] + [(wb[:, k - kh, :], act[:, k, :]) for k in range(kh, KF)]
            mm_group(P, ps[:], Bp, pairs, reads=[Bwa, Bwb] + Bact[0:KF])
            P.dma("sp", xt[i][:], t["x1T"][m * 128:(m + 1) * 128, ts], reads=[Bx1d], writes=[Bxt[i]])
            P.op("dve", lambda E, i=i, ps=ps: E.tensor_tensor(out=x1[i][:], in0=ps[:], in1=xt[i][:], op=ALU.add),
                 reads=[Bp, Bxt[i]], writes=[Bx1[i]])
            P.dma("sp", t["outT"][m * 128:(m + 1) * 128, ts], x1[i][:], reads=[Bx1[i]], writes=[Boutd])
            if last:
                P.op("act", lambda E, i=i: E.activation(sq[i][:], x1[i][:], AF.Square), reads=[Bx1[i]], writes=[Bsq[i]])
                if m == 0:
                    P.op("dve", lambda E, i=i: E.tensor_copy(acc[:], sq[i][:]), reads=[Bsq[i]], writes=[Bacc])
                else:
                    P.op("dve", lambda E, i=i: E.tensor_tensor(out=acc[:], in0=acc[:], in1=sq[i][:], op=ALU.add),
                         reads=[Bsq[i], Bacc], writes=[Bacc])
        if last:
            ps, Bp = cx.psum()
            mm_group(P, ps[:], Bp, [(cx.ones[:], acc[:])], reads=[cx.Bones, Bacc])
            P.op("act", lambda E, ps=ps: E.activation(rstd[:], ps[:], AF.Sqrt, bias=EPS, scale=1.0 / D), reads=[Bp], writes=[Brstd])
            P.op("dve", lambda E: E.reciprocal(rstd[:], rstd[:]), reads=[Brstd], writes=[Brstd])
            for m in range(D // 128):
                i = it % NTMP; it += 1
                P.dma("sp", xt[i][:], t["outT"][m * 128:(m + 1) * 128, ts], reads=[Boutd], writes=[Bxt[i]])
                P.op("dve", lambda E, i=i, m=m: E.scalar_tensor_tensor(out=x1[i][:], in0=xt[i][:], scalar=vec[:, KD + m:KD + m + 1],
                                                                  in1=rstd[:], op0=ALU.mult, op1=ALU.mult),
                     reads=[Bxt[i], Brstd, Bvec], writes=[Bx1[i]])
                P.dma("sp", t["outT"][m * 128:(m + 1) * 128, ts], x1[i][:], reads=[Bx1[i]], writes=[Boutd])


def pk(v):
    v = np.asarray(v, np.float32)
    return np.ascontiguousarray(v.reshape(-1, 128).T)


def build_ffn(cfg, last):
    nc = bass.Bass("TRN2", target_bir_lowering=False)
    D, FF, NT, DMIX = cfg.D, cfg.FF, cfg.NT, cfg.DMIX
    t = {}
    t["xT"] = nc.dram_tensor("xT", [D, NT], F32, kind="ExternalInput").ap()
    t["mixT"] = nc.dram_tensor("mixT", [DMIX, NT], BF16, kind="ExternalInput").ap()
    t["wout"] = nc.dram_tensor("wout", [cfg.KD, 128, cfg.KM, 128], F32, kind="ExternalInput").ap()
    t["wg"] = nc.dram_tensor("wg", [cfg.KF, 128, cfg.KD, 128], F32, kind="ExternalInput").ap()
    t["wu"] = nc.dram_tensor("wu", [cfg.KF, 128, cfg.KD, 128], F32, kind="ExternalInput").ap()
    t["wd"] = nc.dram_tensor("wd", [cfg.KD, 128, cfg.KF, 128], F32, kind="ExternalInput").ap()
    t["vec"] = nc.dram_tensor("vec", [128, 2 * cfg.KD], F32, kind="ExternalInput").ap()
    t["x1T"] = nc.dram_tensor("x1T", [D, NT], F32, kind="Internal").ap()
    t["outT"] = nc.dram_tensor("outT", [D, NT], F32, kind="ExternalOutput").ap()
    P = Prog(nc)
    cx = Ctx(nc, P, cfg)
    emit_ffn_stage(cx, t, last)
    P.finish()
    P.emit()
    return nc, P


HB = 30
C_ID, C_LI, C_NSL, C_E4, C_RST, C_ONE, C_NEG = 0, 128, 256, 384, 388, 900, 1028
NCST = 1156


def vecm_layout(KD):
    o = {}
    c = 0
    for name, n in (("nw1", KD), ("scw", 6), ("gcw", 24), ("alog", 2), ("dtb", 2), ("gnw", 1), ("pscale", 2),
                    ("psel", 4), ("pinvw", 1), ("pinvtab", 16), ("cfw", 248), ("cfb", 8), ("lnw", 8), ("lnb", 8)):
        o[name] = c
        c += n
    o["_n"] = c
    return o


def make_cst():
    c = np.zeros((128, NCST), np.float32)
    i = np.arange(128)
    c[:, C_ID:C_ID + 128] = np.eye(128)
    c[:, C_LI:C_LI + 128] = (i[:, None] >= i[None, :])
    c[:, C_NSL:C_NSL + 128] = -1.0 * (i[:, None] > i[None, :])
    c[0:4, C_E4:C_E4 + 4] = np.eye(4)
    r = np.ones(512, np.float32); r[::128] = 0
    c[0, C_RST:C_RST + 512] = r
    c[0, C_ONE:C_ONE + 128] = 1.0
    c[0, C_NEG:C_NEG + 128] = -1.0
    return c


def norm_block(cx, xsrc, subs, nwcols, hT, BhT, rstd, Brstd, tmp):
    nc, P, cfg = cx.nc, cx.P, cx.cfg
    KD, D = cfg.KD, cfg.D
    KG = min(4, KD)
    xg, Bxg, sqg, Bsqg = tmp
    xr = xsrc.rearrange("(k p) n -> p k n", p=128)
    it = 0
    for (s0, n, off) in subs:
        ps, Bp = cx.psum()
        for g in range(KD // KG):
            i = it % 2; it += 1
            k0 = g * KG
            P.dma("sp", xg[i][:, :, 0:n], xr[:, k0:k0 + KG, s0:s0 + n], writes=[Bxg[i]])
            P.op("act", lambda E, i=i, n=n: E.activation(sqg[i][:, :, 0:n], xg[i][:, :, 0:n], AF.Square),
                 reads=[Bxg[i]], writes=[Bsqg[i]])

            def fn(E, i=i, n=n, g=g, ps=ps):
                ins = None
                for k in range(KG):
                    ins = E.matmul(ps[:, 0:n], cx.ones16[:], sqg[i][:, k, 0:n], start=(g == 0 and k == 0),
                                   stop=(g == KD // KG - 1 and k == KG - 1))
                return ins
            P.op("pe", fn, reads=[Bsqg[i], cx.Bones], writes=[Bp])
            P.op("dve", lambda E, i=i, n=n, k0=k0, off=off: E.tensor_tensor(
                out=hT[:, k0:k0 + KG, off:off + n], in0=xg[i][:, :, 0:n],
                in1=nwcols[:, k0:k0 + KG].unsqueeze(2).to_broadcast([128, KG, n]), op=ALU.mult),
                reads=[Bxg[i], cx.Bvec], writes=[BhT])
        P.op("act", lambda E, ps=ps, n=n, off=off: E.activation(rstd[:, off:off + n], ps[:, 0:n], AF.Sqrt, bias=EPS, scale=1.0 / D),
             reads=[Bp], writes=[Brstd])
        P.op("dve", lambda E, n=n, off=off: E.reciprocal(rstd[:, off:off + n], rstd[:, off:off + n]), reads=[Brstd], writes=[Brstd])


def emit_mixer_gemm(cx, t):
    nc, P, cfg = cx.nc, cx.P, cx.cfg
    D, S, KD, NT = cfg.D, cfg.S, cfg.KD, cfg.NT
    L = cx.L
    BLK = 1024
    W = BLK + 32
    with nc.sbuf_tensor("m_hT", [128, KD, W], BF16) as hT, \
            nc.sbuf_tensor("m_rstd", [128, W], F32) as rstd, \
            nc.sbuf_tensor("m_xg0", [128, min(4, KD), 512], F32) as xg0, \
            nc.sbuf_tensor("m_xg1", [128, min(4, KD), 512], F32) as xg1, \
            nc.sbuf_tensor("m_sq0", [128, min(4, KD), 512], BF16) as sq0, \
            nc.sbuf_tensor("m_sq1", [128, min(4, KD), 512], BF16) as sq1, \
            nc.sbuf_tensor("m_wab", [128, KD, 4], BF16) as wab, \
            nc.sbuf_tensor("m_ev", [128, 6, 512], F32) as ev:
        BhT, Brstd, Bwab = Buf("hT"), Buf("rstd"), Buf("wab")
        Bev = [Buf() for _ in range(6)]
        tmp = ([xg0, xg1], [Buf(), Buf()], [sq0, sq1], [Buf(), Buf()])
        nw = cx.vec[:, L["nw1"]:L["nw1"] + KD]
        P.dma("pool", wab[:], t["w_ab"], writes=[Bwab])
        BpACD, Bpab, Bglu = cx.Bscr["pACD"], cx.Bscr["pab"], cx.Bscr["glu"]
        ei = 0
        for b0 in range(0, S, BLK):
            nb = min(BLK, S - b0)
            subs = [(b0 + o, min(512, nb - o), o) for o in range(0, nb, 512)]
            norm_block(cx, t["xTf"], subs, nw, hT, BhT, rstd, Brstd, tmp)
            for m in range(16):
                w, Bw = cx.wtile(t["w_acd"][m], KD)
                for (s0, n, off) in subs:
                    ps, Bp = cx.psum()
                    mm_group(P, ps[:, 0:n], Bp, [(w[:, k, :], hT[:, k, off:off + n]) for k in range(KD)], reads=[Bw, BhT])
                    e = ei % 6; ei += 1
                    P.op("dve", lambda E, e=e, ps=ps, n=n, off=off: E.tensor_tensor(out=ev[:, e, 0:n], in0=ps[:, 0:n], in1=rstd[:, off:off + n], op=ALU.mult),
                         reads=[Bp, Brstd], writes=[Bev[e]])
                    P.dma("sp", t["pACD"][m * 128:(m + 1) * 128, s0:s0 + n], ev[:, e, 0:n], reads=[Bev[e]], writes=[BpACD])
            for (s0, n, off) in subs:
                ps, Bp = cx.psum()
                mm_group(P, ps[0:4, 0:n], Bp, [(wab[:, k, :], hT[:, k, off:off + n]) for k in range(KD)], reads=[Bwab, BhT])
                e = ei % 6; ei += 1
                P.op("dve", lambda E, e=e, ps=ps, n=n, off=off: E.tensor_tensor(out=ev[0:4, e, 0:n], in0=ps[0:4, 0:n], in1=rstd[0:4, off:off + n], op=ALU.mult),
                     reads=[Bp, Brstd], writes=[Bev[e]])
                P.dma("sp", t["pab"][:, s0:s0 + n], ev[0:4, e, 0:n], reads=[Bev[e]], writes=[Bpab])
        tot = HB + NT
        for b0 in range(0, tot, BLK + HB):
            nb = min(BLK + HB, tot - b0)
            subs = []
            o = 0
            if b0 == 0:
                subs.append((0, HB, 0)); o = HB
            while o < nb:
                n = min(512, nb - o)
                subs.append((b0 + o, n, o)); o += n
            norm_block(cx, t["xTq"], subs, nw, hT, BhT, rstd, Brstd, tmp)
            for m in range(8):
                wv, Bwv = cx.wtile(t["w_b"][m], KD)
                wg, Bwg = cx.wtile(t["w_b"][m + 8], KD)
                for (s0, n, off) in subs:
                    pv, Bpv = cx.psum()
                    mm_group(P, pv[:, 0:n], Bpv, [(wv[:, k, :], hT[:, k, off:off + n]) for k in range(KD)], reads=[Bwv, BhT])
                    pg, Bpg = cx.psum()
                    mm_group(P, pg[:, 0:n], Bpg, [(wg[:, k, :], hT[:, k, off:off + n]) for k in range(KD)], reads=[Bwg, BhT])
                    e = ei % 6; e2 = (ei + 1) % 6; ei += 2
                    P.op("dve", lambda E, e=e, pv=pv, n=n, off=off: E.tensor_tensor(out=ev[:, e, 0:n], in0=pv[:, 0:n], in1=rstd[:, off:off + n], op=ALU.mult),
                         reads=[Bpv, Brstd], writes=[Bev[e]])
                    P.op("dve", lambda E, e2=e2, pg=pg, n=n, off=off: E.tensor_tensor(out=ev[:, e2, 0:n], in0=pg[:, 0:n], in1=rstd[:, off:off + n], op=ALU.mult),
                         reads=[Bpg, Brstd], writes=[Bev[e2]])
                    P.op("act", lambda E, e2=e2, n=n: E.activation(ev[:, e2, 0:n], ev[:, e2, 0:n], AF.Sigmoid), reads=[Bev[e2]], writes=[Bev[e2]])
                    P.op("dve", lambda E, e=e, e2=e2, n=n: E.tensor_tensor(out=ev[:, e, 0:n], in0=ev[:, e, 0:n], in1=ev[:, e2, 0:n], op=ALU.mult),
                         reads=[Bev[e], Bev[e2]], writes=[Bev[e]])
                    P.dma("sp", t["glu"][m * 128:(m + 1) * 128, s0:s0 + n], ev[:, e, 0:n], reads=[Bev[e]], writes=[Bglu])
    P.barrier()


def load_win(cx, dst, Bdst, src_rows, nch, t0, H, n=512, eng="sp"):
    P = cx.P
    sr = src_rows.rearrange("(c p) n -> p c n", p=128)
    if t0 == 0 and H > 0:
        P.op("dve", lambda E: E.memset(dst[:, 0:nch, 0:H], 0.0), writes=[Bdst])
        P.dma(eng, dst[:, 0:nch, H:H + n], sr[:, :, 0:n], writes=[Bdst])
    else:
        P.dma(eng, dst[:, 0:nch, 0:H + n], sr[:, :, t0 - H:t0 + n], writes=[Bdst])


def bc(ap, shape):
    return ap.to_broadcast(shape)


def emit_mixer_acd(cx, t):
    nc, P, cfg = cx.nc, cx.P, cx.cfg
    S = cfg.S
    L = cx.L
    vec, Bvec, cst, Bcst = cx.vec, cx.Bvec, cx.cst, cx.Bcst
    A_ = nc.alloc_sbuf_tensor

    def V(name, i=0, n=1):
        return vec[:, L[name] + i:L[name] + i + n]
    from contextlib import ExitStack
    with ExitStack() as es:
        def T(name, shape, dt=F32):
            return es.enter_context(nc.sbuf_tensor("a_" + name, shape, dt))
        scb = T("scb", [128, 2, 512]); scc = T("scc", [128, 2, 514]); sch = T("sch", [128, 2, 514])
        prod = T("prod", [128, 2, 514]); accA = T("accA", [128, 2, 512]); yA = T("yA", [128, 2, 512], BF16)
        pu = T("pu", [128, 2, 527]); sa = T("sa", [128, 2, 527]); sb = T("sb", [128, 2, 527]); ssum = T("ssum", [128, 2, 512])
        pbf = T("pbf", [128, 2, 512], BF16); t16 = T("t16", [128, 2, 16]); yD = T("yD", [128, 2, 512], BF16)
        poolw = T("poolw", [128, 2, 256], BF16)
        raw = T("raw", [128, 6, 515]); qkv = T("qkv", [128, 6, 512]); zt = T("zt", [128, 2, 512])
        sqn = T("sqn", [128, 512]); rn = T("rn", [128, 512])
        ab = T("ab", [4, 512]); rows = T("rows", [1, 2, 5, 512]); colv = T("colv", [128, 8, 4])
        negA = T("negA", [128, 2]); Sst = T("Sst", [128, 2, 128]); yC = T("yC", [128, 2, 512], BF16)
        M = {n: T("g_" + n, [128, 4, 128]) for n in
             ("kg", "kd", "vt", "tm", "Em", "A0", "A1", "B0", "B1", "X0", "X1", "Y0", "Y1", "AI", "AT", "TB", "u", "wT")}
        r2 = {n: T("r_" + n, [128, 2, 128]) for n in ("vn", "o1", "o", "osq", "on", "st")}
        ss2 = T("ss2", [128, 2])
        B = {n: Buf(n) for n in ("scb", "scc", "sch", "prod", "accA", "yA", "pu", "sa", "sb", "ssum", "pbf", "t16", "yD", "poolw",
                                 "raw", "qkv", "zt", "sqn", "rn", "ab", "rows", "colv", "negA", "S", "yC", "ss2")}
        BM = {n: Buf(n) for n in M}
        Br = {n: Buf(n) for n in r2}
        pACD, pab, yACD = t["pACD"], t["pab"], t["yACD"]
        ByACD = cx.Bscr["yACD"]
        BpACD, Bpab = cx.Bscr["pACD"], cx.Bscr["pab"]
        ident = cst[:, C_ID:C_ID + 128]
        ident4 = bc(cst[:, C_ID:C_ID + 128].unsqueeze(1), [128, 4, 128])
        li4 = bc(cst[:, C_LI:C_LI + 128].unsqueeze(1), [128, 4, 128])
        nsl4 = bc(cst[:, C_NSL:C_NSL + 128].unsqueeze(1), [128, 4, 128])
        one11 = cst[0:1, C_ONE:C_ONE + 1]
        onesrow = cst[0:1, C_ONE:C_ONE + 128]
        negrow = cst[0:1, C_NEG:C_NEG + 128]
        rst = cst[0:1, C_RST:C_RST + 512]

        P.dma("pool", poolw[:], t["poolw"], writes=[B["poolw"]])
        P.op("act", lambda E: E.activation(negA[:], V("alog", 0, 2), AF.Exp), reads=[Bvec], writes=[B["negA"]])
        P.op("dve", lambda E: E.tensor_scalar(out=negA[:], in0=negA[:], scalar1=-1.0, scalar2=None, op0=ALU.mult), reads=[B["negA"]], writes=[B["negA"]])
        P.op("dve", lambda E: E.memset(Sst[:], 0.0), writes=[B["S"]])

        def pe_batch(ps, fn_pairs, reads, Bp, n=4, transpose=False):
            def fn(E):
                ins = None
                for p in range(n):
                    o = ps[:, p * 128:(p + 1) * 128]
                    if transpose:
                        ins = E.transpose(o, fn_pairs[p], ident)
                    else:
                        groups = fn_pairs[p]
                        for gi, (l, r) in enumerate(groups):
                            ins = E.matmul(o, l, r, start=(gi == 0), stop=(gi == len(groups) - 1))
                return ins
            return P.op("pe", fn, reads=reads + ([Bcst] if transpose else []), writes=[Bp])

        for tt in range(S // 512):
            t0 = tt * 512
            ts = slice(t0, t0 + 512)
            P.dma("sp", scb[:], pACD[0:256, ts].rearrange("(c p) n -> p c n", p=128), reads=[BpACD], writes=[B["scb"]])
            load_win(cx, scc, B["scc"], pACD[256:512, :], 2, t0, 2)
            load_win(cx, sch, B["sch"], pACD[512:768, :], 2, t0, 2)
            load_win(cx, raw, B["raw"], pACD[768:1536, :], 6, t0, 3)
            P.dma("sp", zt[:], pACD[1536:1792, ts].rearrange("(c p) n -> p c n", p=128), writes=[B["zt"]])
            load_win(cx, pu, B["pu"], pACD[1792:2048, :], 2, t0, 15)
            P.dma("sp", ab[:], pab[:, ts], reads=[Bpab], writes=[B["ab"]])
            P.op("dve", lambda E: E.tensor_tensor(out=prod[:], in0=scc[:], in1=sch[:], op=ALU.mult), reads=[B["scc"], B["sch"]], writes=[B["prod"]])
            for c in range(2):
                P.op("dve", lambda E, c=c: E.tensor_scalar(out=accA[:, c, :], in0=prod[:, c, 0:512], scalar1=V("scw", c * 3), scalar2=None, op0=ALU.mult),
                     reads=[B["prod"], Bvec], writes=[B["accA"]])
                for tap in (1, 2):
                    P.op("dve", lambda E, c=c, tap=tap: E.scalar_tensor_tensor(out=accA[:, c, :], in0=prod[:, c, tap:tap + 512], scalar=V("scw", c * 3 + tap),
                                                                            in1=accA[:, c, :], op0=ALU.mult, op1=ALU.add),
                         reads=[B["prod"], B["accA"], Bvec], writes=[B["accA"]])
            P.op("dve", lambda E: E.tensor_tensor(out=yA[:], in0=accA[:], in1=scb[:], op=ALU.mult), reads=[B["accA"], B["scb"]], writes=[B["yA"]])
            P.dma("sp", yACD[0:256, ts].rearrange("(c p) n -> p c n", p=128), yA[:], reads=[B["yA"]], writes=[ByACD])
            P.op("dve", lambda E: E.tensor_tensor(out=sa[:, :, 1:527], in0=pu[:, :, 1:527], in1=pu[:, :, 0:526], op=ALU.add), reads=[B["pu"]], writes=[B["sa"]])
            P.op("dve", lambda E: E.tensor_scalar(out=ssum[:], in0=sa[:, :, 15:527], scalar1=V("psel", 0), scalar2=None, op0=ALU.mult),
                 reads=[B["sa"], Bvec], writes=[B["ssum"]])
            P.op("dve", lambda E: E.tensor_tensor(out=sb[:, :, 3:527], in0=sa[:, :, 3:527], in1=sa[:, :, 1:525], op=ALU.add), reads=[B["sa"]], writes=[B["sb"]])
            P.op("dve", lambda E: E.scalar_tensor_tensor(out=ssum[:], in0=sb[:, :, 15:527], scalar=V("psel", 1), in1=ssum[:], op0=ALU.mult, op1=ALU.add),
                 reads=[B["sb"], B["ssum"], Bvec], writes=[B["ssum"]])
            P.op("dve", lambda E: E.tensor_tensor(out=sa[:, :, 7:527], in0=sb[:, :, 7:527], in1=sb[:, :, 3:523], op=ALU.add), reads=[B["sb"]], writes=[B["sa"]])
            P.op("dve", lambda E: E.scalar_tensor_tensor(out=ssum[:], in0=sa[:, :, 15:527], scalar=V("psel", 2), in1=ssum[:], op0=ALU.mult, op1=ALU.add),
                 reads=[B["sa"], B["ssum"], Bvec], writes=[B["ssum"]])
            P.op("dve", lambda E: E.tensor_tensor(out=sb[:, :, 15:527], in0=sa[:, :, 15:527], in1=sa[:, :, 7:519], op=ALU.add), reads=[B["sa"]], writes=[B["sb"]])
            P.op("dve", lambda E: E.scalar_tensor_tensor(out=ssum[:], in0=sb[:, :, 15:527], scalar=V("psel", 3), in1=ssum[:], op0=ALU.mult, op1=ALU.add),
                 reads=[B["sb"], B["ssum"], Bvec], writes=[B["ssum"]])
            P.op("dve", lambda E: E.scalar_tensor_tensor(out=pbf[:], in0=ssum[:], scalar=V("pinvw"), in1=pu[:, :, 15:527], op0=ALU.mult, op1=ALU.subtract),
                 reads=[B["ssum"], B["pu"], Bvec], writes=[B["pbf"]])
            if t0 == 0:
                P.op("dve", lambda E: E.tensor_tensor(out=t16[:], in0=ssum[:, :, 0:16], in1=bc(V("pinvtab", 0, 16).unsqueeze(1), [128, 2, 16]), op=ALU.mult),
                     reads=[B["ssum"], Bvec], writes=[B["t16"]])
                P.op("dve", lambda E: E.tensor_tensor(out=pbf[:, :, 0:16], in0=t16[:], in1=pu[:, :, 15:31], op=ALU.subtract),
                     reads=[B["t16"], B["pu"]], writes=[B["pbf"]])
            for dm in range(2):
                ps, Bp = cx.psum()
                mm_group(P, ps[:], Bp, [(poolw[:, ck, dm * 128:(dm + 1) * 128], pbf[:, ck, :]) for ck in range(2)], reads=[B["poolw"], B["pbf"]])
                P.op("act", lambda E, dm=dm, ps=ps: E.activation(yD[:, dm, :], ps[:], AF.Copy, scale=V("pscale", dm)), reads=[Bp, Bvec], writes=[B["yD"]])
            P.dma("sp", yACD[512:768, ts].rearrange("(c p) n -> p c n", p=128), yD[:], reads=[B["yD"]], writes=[ByACD])
            for c in range(6):
                P.op("dve", lambda E, c=c: E.tensor_scalar(out=qkv[:, c, :], in0=raw[:, c, 0:512], scalar1=V("gcw", c * 4), scalar2=None, op0=ALU.mult),
                     reads=[B["raw"], Bvec], writes=[B["qkv"]])
                for tap in (1, 2, 3):
                    P.op("dve", lambda E, c=c, tap=tap: E.scalar_tensor_tensor(out=qkv[:, c, :], in0=raw[:, c, tap:tap + 512], scalar=V("gcw", c * 4 + tap),
                                                                            in1=qkv[:, c, :], op0=ALU.mult, op1=ALU.add),
                         reads=[B["raw"], B["qkv"], Bvec], writes=[B["qkv"]])
            P.op("act", lambda E: E.activation(qkv[:], qkv[:], AF.Silu), reads=[B["qkv"]], writes=[B["qkv"]])
            P.op("act", lambda E: E.activation(zt[:], zt[:], AF.Silu), reads=[B["zt"]], writes=[B["zt"]])
            for c in range(4):
                P.op("act", lambda E, c=c: E.activation(sqn[:], qkv[:, c, :], AF.Square), reads=[B["qkv"]], writes=[B["sqn"]])
                ps, Bp = cx.psum()
                mm_group(P, ps[:], Bp, [(cx.ones[:], sqn[:])], reads=[cx.Bones, B["sqn"]])
                P.op("act", lambda E, ps=ps: E.activation(rn[:], ps[:], AF.Sqrt, bias=EPS, scale=1.0), reads=[Bp], writes=[B["rn"]])
                P.op("dve", lambda E: E.reciprocal(rn[:], rn[:]), reads=[B["rn"]], writes=[B["rn"]])
                sc_ = (128.0 ** -0.5) if c < 2 else 1.0
                P.op("dve", lambda E, c=c, sc_=sc_: E.scalar_tensor_tensor(out=qkv[:, c, :], in0=qkv[:, c, :], scalar=sc_, in1=rn[:], op0=ALU.mult, op1=ALU.mult),
                     reads=[B["qkv"], B["rn"]], writes=[B["qkv"]])
            for h in range(2):
                psa, Bpa = cx.psum()
                mm_group(P, psa[0:1, :], Bpa, [(cst[0:4, C_E4 + h:C_E4 + h + 1], ab[0:4, :])], reads=[Bcst, B["ab"]])
                psb, Bpb = cx.psum()
                mm_group(P, psb[0:1, :], Bpb, [(cst[0:4, C_E4 + 2 + h:C_E4 + 3 + h], ab[0:4, :])], reads=[Bcst, B["ab"]])
                R = lambda i, h=h: rows[0:1, h, i, :]
                P.op("act", lambda E, psa=psa, h=h: E.activation(rows[0:1, h, 0, :], psa[0:1, :], AF.Exp, bias=vec[0:1, L["dtb"] + h:L["dtb"] + h + 1]),
                     reads=[Bpa, Bvec], writes=[B["rows"]])
                P.op("act", lambda E, h=h: E.activation(rows[0:1, h, 0, :], rows[0:1, h, 0, :], AF.Ln, bias=1.0), reads=[B["rows"]], writes=[B["rows"]])
                P.op("dve", lambda E, h=h: E.tensor_scalar(out=rows[0:1, h, 0, :], in0=rows[0:1, h, 0, :], scalar1=negA[0:1, h:h + 1], scalar2=None, op0=ALU.mult),
                     reads=[B["rows"], B["negA"]], writes=[B["rows"]])
                P.op("dve", lambda E, h=h: E.tensor_tensor_scan(out=rows[0:1, h, 1, :], data0=rst, data1=rows[0:1, h, 0, :], initial=0.0, op0=ALU.mult, op1=ALU.add),
                     reads=[B["rows"], Bcst], writes=[B["rows"]])
                P.op("act", lambda E, h=h: E.activation(rows[0:1, h, 2, :], rows[0:1, h, 1, :], AF.Exp), reads=[B["rows"]], writes=[B["rows"]])
                P.op("dve", lambda E, h=h: E.tensor_tensor(out=rows[0:1, h, 3, :].rearrange("p (a b) -> p a b", b=128),
                                                          in0=bc(rows[0:1, h, 1, :].rearrange("p (a b) -> p a b", b=128)[:, :, 127:128], [1, 4, 128]),
                                                          in1=rows[0:1, h, 1, :].rearrange("p (a b) -> p a b", b=128), op=ALU.subtract),
                     reads=[B["rows"]], writes=[B["rows"]])
                P.op("act", lambda E, h=h: E.activation(rows[0:1, h, 3, :], rows[0:1, h, 3, :], AF.Exp), reads=[B["rows"]], writes=[B["rows"]])
                P.op("act", lambda E, psb=psb, h=h: E.activation(rows[0:1, h, 4, :], psb[0:1, :], AF.Sigmoid), reads=[Bpb], writes=[B["rows"]])
            psc, Bpc = cx.psum()

            def fn_cols(E, psc=psc):
                ins = None
                for sc in range(4):
                    for h in range(2):
                        p8 = sc * 2 + h
                        cs = sc * 128
                        for q, ri in ((0, 4), (1, 2), (2, 3)):
                            ins = E.matmul(psc[:, p8 * 4 + q:p8 * 4 + q + 1], rows[0:1, h, ri, cs:cs + 128], one11, start=True, stop=True)
                        ins = E.matmul(psc[:, p8 * 4 + 3:p8 * 4 + 4], onesrow, rows[0:1, h, 2, cs + 127:cs + 128], start=True, stop=True)
                return ins
            P.op("pe", fn_cols, reads=[B["rows"], Bcst], writes=[Bpc])
            P.op("dve", lambda E, psc=psc: E.tensor_copy(colv[:].rearrange("p a b -> p (a b)"), psc[:, 0:32]), reads=[Bpc], writes=[B["colv"]])
            for b2 in range(2):
                def prob(p):
                    scl, h = p // 2, p % 2
                    sc = 2 * b2 + scl
                    return sc, h, sc * 128
                pp = slice(b2 * 4, b2 * 4 + 4)

                def CV(q, pp=pp):
                    return bc(colv[:, pp, q:q + 1], [128, 4, 128])
                qn = [qkv[:, 0 + prob(p)[1], prob(p)[2]:prob(p)[2] + 128] for p in range(4)]
                kn = [qkv[:, 2 + prob(p)[1], prob(p)[2]:prob(p)[2] + 128] for p in range(4)]
                vn_ = [qkv[:, 4 + prob(p)[1], prob(p)[2]:prob(p)[2] + 128] for p in range(4)]
                gcr = [rows[0:1, prob(p)[1], 1, prob(p)[2]:prob(p)[2] + 128] for p in range(4)]
                m3 = lambda n: M[n][:].rearrange("p a b -> p (a b)")
                ps, Bp = cx.psum()
                pe_batch(ps, kn, [B["qkv"]], Bp, transpose=True)
                psv = ps[:].rearrange("p (a b) -> p a b", b=128)
                P.op("dve", lambda E, psv=psv, CV=CV: E.tensor_tensor(out=M["kg"][:], in0=psv, in1=CV(1), op=ALU.mult), reads=[Bp, B["colv"]], writes=[BM["kg"]])
                P.op("dve", lambda E, psv=psv, CV=CV: E.tensor_tensor(out=M["kd"][:], in0=psv, in1=CV(2), op=ALU.mult), reads=[Bp, B["colv"]], writes=[BM["kd"]])
                ps, Bp = cx.psum()
                pe_batch(ps, vn_, [B["qkv"]], Bp, transpose=True)
                P.op("act", lambda E, ps=ps: E.activation(m3("vt"), ps[:], AF.Copy), reads=[Bp], writes=[BM["vt"]])
                ps, Bp = cx.psum()
                pe_batch(ps, [[(gcr[p], onesrow), (negrow, gcr[p])] for p in range(4)], [B["rows"], Bcst], Bp)
                P.op("dve", lambda E, ps=ps: E.tensor_scalar(out=m3("Em"), in0=ps[:], scalar1=0.0, scalar2=None, op0=ALU.min), reads=[Bp], writes=[BM["Em"]])
                P.op("act", lambda E: E.activation(m3("Em"), m3("Em"), AF.Exp), reads=[BM["Em"]], writes=[BM["Em"]])
                P.op("dve", lambda E: E.tensor_tensor(out=M["Em"][:], in0=M["Em"][:], in1=li4, op=ALU.mult), reads=[BM["Em"], Bcst], writes=[BM["Em"]])
                ps, Bp = cx.psum()
                pe_batch(ps, [[(kn[p], kn[p])] for p in range(4)], [B["qkv"]], Bp)
                P.op("dve", lambda E, ps=ps: E.tensor_tensor(out=m3("tm"), in0=ps[:], in1=m3("Em"), op=ALU.mult), reads=[Bp, BM["Em"]], writes=[BM["tm"]])
                P.op("dve", lambda E, CV=CV: E.tensor_tensor(out=M["tm"][:], in0=M["tm"][:], in1=CV(0), op=ALU.mult), reads=[BM["tm"], B["colv"]], writes=[BM["tm"]])
                P.op("dve", lambda E: E.tensor_tensor(out=M["A0"][:], in0=M["tm"][:], in1=nsl4, op=ALU.mult), reads=[BM["tm"], Bcst], writes=[BM["A0"]])
                ps, Bp = cx.psum()
                pe_batch(ps, [M["A0"][:, p, :] for p in range(4)], [BM["A0"]], Bp, transpose=True)
                P.op("act", lambda E, ps=ps: E.activation(m3("B0"), ps[:], AF.Copy), reads=[Bp], writes=[BM["B0"]])
                ps, Bp = cx.psum()
                pe_batch(ps, [[(qn[p], kn[p])] for p in range(4)], [B["qkv"]], Bp)
                P.op("dve", lambda E, ps=ps: E.tensor_tensor(out=m3("AI"), in0=ps[:], in1=m3("Em"), op=ALU.mult), reads=[Bp, BM["Em"]], writes=[BM["AI"]])
                ps, Bp = cx.psum()
                pe_batch(ps, [M["AI"][:, p, :] for p in range(4)], [BM["AI"]], Bp, transpose=True)
                P.op("act", lambda E, ps=ps: E.activation(m3("AT"), ps[:], AF.Copy), reads=[Bp], writes=[BM["AT"]])
                P.op("dve", lambda E: E.tensor_tensor(out=M["X0"][:], in0=M["B0"][:], in1=ident4, op=ALU.add), reads=[BM["B0"], Bcst], writes=[BM["X0"]])
                P.op("dve", lambda E: E.tensor_tensor(out=M["Y0"][:], in0=M["A0"][:], in1=ident4, op=ALU.add), reads=[BM["A0"], Bcst], writes=[BM["Y0"]])
                cur = 0
                for lvl in range(1, 7):
                    a_o, a_n = f"A{cur}", f"A{1 - cur}"
                    b_o, b_n = f"B{cur}", f"B{1 - cur}"
                    x_o, x_n = f"X{cur}", f"X{1 - cur}"
                    y_o, y_n = f"Y{cur}", f"Y{1 - cur}"
                    ps, Bp = cx.psum()
                    pe_batch(ps, [[(M[a_o][:, p, :], M[b_o][:, p, :])] for p in range(4)], [BM[a_o], BM[b_o]], Bp)
                    P.op("act", lambda E, ps=ps, b_n=b_n: E.activation(m3(b_n), ps[:], AF.Copy), reads=[Bp], writes=[BM[b_n]])
                    if lvl < 6:
                        ps, Bp = cx.psum()
                        pe_batch(ps, [[(M[b_o][:, p, :], M[a_o][:, p, :])] for p in range(4)], [BM[a_o], BM[b_o]], Bp)
                        P.op("act", lambda E, ps=ps, a_n=a_n: E.activation(m3(a_n), ps[:], AF.Copy), reads=[Bp], writes=[BM[a_n]])
                    ps, Bp = cx.psum()
                    pe_batch(ps, [[(M[y_o][:, p, :], M[b_n][:, p, :])] for p in range(4)], [BM[y_o], BM[b_n]], Bp)
                    P.op("dve", lambda E, ps=ps, x_o=x_o, x_n=x_n: E.tensor_tensor(out=m3(x_n), in0=ps[:], in1=m3(x_o), op=ALU.add), reads=[Bp, BM[x_o]], writes=[BM[x_n]])
                    if lvl < 6:
                        ps, Bp = cx.psum()
                        pe_batch(ps, [[(M[x_o][:, p, :], M[a_n][:, p, :])] for p in range(4)], [BM[x_o], BM[a_n]], Bp)
                        P.op("dve", lambda E, ps=ps, y_o=y_o, y_n=y_n: E.tensor_tensor(out=m3(y_n), in0=ps[:], in1=m3(y_o), op=ALU.add), reads=[Bp, BM[y_o]], writes=[BM[y_n]])
                    cur = 1 - cur
                xf = f"X{cur}"
                P.op("dve", lambda E, xf=xf, CV=CV: E.tensor_tensor(out=M["TB"][:], in0=M[xf][:], in1=CV(0), op=ALU.mult), reads=[BM[xf], B["colv"]], writes=[BM["TB"]])
                ps, Bp = cx.psum()
                pe_batch(ps, [[(M["TB"][:, p, :], M["vt"][:, p, :])] for p in range(4)], [BM["TB"], BM["vt"]], Bp)
                P.op("act", lambda E, ps=ps: E.activation(m3("u"), ps[:], AF.Copy), reads=[Bp], writes=[BM["u"]])
                ps, Bp = cx.psum()
                pe_batch(ps, [[(M["kg"][:, p, :], M["TB"][:, p, :])] for p in range(4)], [BM["TB"], BM["kg"]], Bp)
                P.op("act", lambda E, ps=ps: E.activation(m3("wT"), ps[:], AF.Copy), reads=[Bp], writes=[BM["wT"]])
                for scl in range(2):
                    sc = 2 * b2 + scl
                    cs = sc * 128
                    p2 = slice(scl * 2, scl * 2 + 2)
                    P8 = slice(sc * 2, sc * 2 + 2)
                    r3 = lambda n: r2[n][:].rearrange("p a b -> p (a b)")
                    ps1, Bp1 = cx.psum()
                    pe_batch(ps1, [[(M["wT"][:, scl * 2 + h, :], Sst[:, h, :])] for h in range(2)], [BM["wT"], B["S"]], Bp1, n=2)
                    P.op("dve", lambda E, ps1=ps1, p2=p2: E.tensor_tensor(out=r3("vn"), in0=M["u"][:, p2, :].rearrange("p a b -> p (a b)"), in1=ps1[:, 0:256], op=ALU.subtract),
                         reads=[Bp1, BM["u"]], writes=[Br["vn"]])
                    ps2, Bp2 = cx.psum()
                    pe_batch(ps2, [[(qkv[:, h, cs:cs + 128], Sst[:, h, :])] for h in range(2)], [B["qkv"], B["S"]], Bp2, n=2)
                    ps3, Bp3 = cx.psum()
                    pe_batch(ps3, [[(M["AT"][:, scl * 2 + h, :], r2["vn"][:, h, :])] for h in range(2)], [BM["AT"], Br["vn"]], Bp3, n=2)
                    P.op("dve", lambda E, ps2=ps2, P8=P8: E.tensor_tensor(out=r2["o1"][:], in0=ps2[:, 0:256].rearrange("p (a b) -> p a b", b=128),
                                                                        in1=bc(colv[:, P8, 1:2], [128, 2, 128]), op=ALU.mult),
                         reads=[Bp2, B["colv"]], writes=[Br["o1"]])
                    P.op("dve", lambda E, ps3=ps3: E.tensor_tensor(out=r3("o"), in0=r3("o1"), in1=ps3[:, 0:256], op=ALU.add), reads=[Bp3, Br["o1"]], writes=[Br["o"]])
                    ps4, Bp4 = cx.psum()
                    pe_batch(ps4, [[(M["kd"][:, scl * 2 + h, :], r2["vn"][:, h, :])] for h in range(2)], [BM["kd"], Br["vn"]], Bp4, n=2)
                    P.op("dve", lambda E, P8=P8: E.tensor_tensor(out=r2["st"][:], in0=Sst[:], in1=bc(colv[:, P8, 3:4], [128, 2, 128]), op=ALU.mult),
                         reads=[B["S"], B["colv"]], writes=[Br["st"]])
                    P.op("dve", lambda E, ps4=ps4: E.tensor_tensor(out=Sst[:].rearrange("p a b -> p (a b)"), in0=r3("st"), in1=ps4[:, 0:256], op=ALU.add),
                         reads=[Bp4, Br["st"]], writes=[B["S"]])
                    P.op("dve", lambda E: E.tensor_tensor(out=r2["osq"][:], in0=r2["o"][:], in1=r2["o"][:], op=ALU.mult), reads=[Br["o"]], writes=[Br["osq"]])
                    P.op("dve", lambda E: E.reduce_sum(out=ss2[:], in_=r2["osq"][:], axis=AX.X), reads=[Br["osq"]], writes=[B["ss2"]])
                    P.op("act", lambda E: E.activation(ss2[:], ss2[:], AF.Sqrt, bias=EPS, scale=1.0 / 128), reads=[B["ss2"]], writes=[B["ss2"]])
                    P.op("dve", lambda E: E.reciprocal(ss2[:], ss2[:]), reads=[B["ss2"]], writes=[B["ss2"]])
                    P.op("dve", lambda E: E.tensor_tensor(out=r2["on"][:], in0=r2["o"][:], in1=bc(ss2[:].unsqueeze(2), [128, 2, 128]), op=ALU.mult),
                         reads=[Br["o"], B["ss2"]], writes=[Br["on"]])
                    pst, Bpt = cx.psum()
                    pe_batch(pst, [r2["on"][:, h, :] for h in range(2)], [Br["on"]], Bpt, n=2, transpose=True)
                    P.op("dve", lambda E, pst=pst, cs=cs: E.scalar_tensor_tensor(out=yC[:, :, cs:cs + 128], in0=pst[:, 0:256].rearrange("p (a b) -> p a b", b=128),
                                                                               scalar=V("gnw"), in1=zt[:, :, cs:cs + 128], op0=ALU.mult, op1=ALU.mult),
                         reads=[Bpt, B["zt"], Bvec], writes=[B["yC"]])
            P.dma("sp", yACD[256:512, ts].rearrange("(c p) n -> p c n", p=128), yC[:], reads=[B["yC"]], writes=[ByACD])
        P.barrier()


def emit_mixer_b(cx, t):
    nc, P, cfg = cx.nc, cx.P, cx.cfg
    NT = cfg.NT
    L = cx.L
    vec, Bvec = cx.vec, cx.Bvec

    def V(name, i=0, n=1):
        return vec[:, L[name] + i:L[name] + i + n]
    with nc.sbuf_tensor("b_glu", [128, 8, 512 + HB], F32) as glu, \
            nc.sbuf_tensor("b_cf", [128, 8, 512], F32) as cf, \
            nc.sbuf_tensor("b_sq", [128, 8, 512], F32) as sq8, \
            nc.sbuf_tensor("b_mean", [128, 512], F32) as mean, \
            nc.sbuf_tensor("b_msq", [128, 512], F32) as msq, \
            nc.sbuf_tensor("b_rstd", [128, 512], F32) as rstd, \
            nc.sbuf_tensor("b_y", [128, 8, 512], BF16) as yB:
        Bg, Bcf, Bsq, Bmean, Bmsq, Brstd, By = (Buf() for _ in range(7))
        Bglu, ByB = cx.Bscr["glu"], cx.Bscr["yB"]
        for tt in range(NT // 512):
            t0 = tt * 512
            P.dma("sp", glu[:], t["glu"].rearrange("(c p) n -> p c n", p=128)[:, :, t0:t0 + 512 + HB], reads=[Bglu], writes=[Bg])
            for c in range(8):
                P.op("dve", lambda E, c=c: E.tensor_scalar(out=cf[:, c, :], in0=glu[:, c, 0:512], scalar1=V("cfw", c * 31), scalar2=V("cfb", c), op0=ALU.mult, op1=ALU.add),
                     reads=[Bg, Bvec], writes=[Bcf])
                for tap in range(1, 31):
                    P.op("dve", lambda E, c=c, tap=tap: E.scalar_tensor_tensor(out=cf[:, c, :], in0=glu[:, c, tap:tap + 512], scalar=V("cfw", c * 31 + tap),
                                                                            in1=cf[:, c, :], op0=ALU.mult, op1=ALU.add),
                         reads=[Bg, Bcf, Bvec], writes=[Bcf])
            P.op("act", lambda E: E.activation(sq8[:], cf[:], AF.Square), reads=[Bcf], writes=[Bsq])
            ps1, Bp1 = cx.psum()
            mm_group(P, ps1[:], Bp1, [(cx.ones[:], cf[:, c, :]) for c in range(8)], reads=[cx.Bones, Bcf])
            ps2, Bp2 = cx.psum()
            mm_group(P, ps2[:], Bp2, [(cx.ones[:], sq8[:, c, :]) for c in range(8)], reads=[cx.Bones, Bsq])
            P.op("act", lambda E, ps1=ps1: E.activation(mean[:], ps1[:], AF.Copy, scale=1.0 / DG), reads=[Bp1], writes=[Bmean])
            P.op("dve", lambda E: E.tensor_tensor(out=msq[:], in0=mean[:], in1=mean[:], op=ALU.mult), reads=[Bmean], writes=[Bmsq])
            P.op("dve", lambda E, ps2=ps2: E.scalar_tensor_tensor(out=rstd[:], in0=ps2[:], scalar=1.0 / DG, in1=msq[:], op0=ALU.mult, op1=ALU.subtract),
                 reads=[Bp2, Bmsq], writes=[Brstd])
            P.op("act", lambda E: E.activation(rstd[:], rstd[:], AF.Sqrt, bias=EPS, scale=1.0), reads=[Brstd], writes=[Brstd])
            P.op("dve", lambda E: E.reciprocal(rstd[:], rstd[:]), reads=[Brstd], writes=[Brstd])
            P.op("dve", lambda E: E.tensor_tensor(out=cf[:], in0=cf[:], in1=bc(mean[:].unsqueeze(1), [128, 8, 512]), op=ALU.subtract), reads=[Bcf, Bmean], writes=[Bcf])
            P.op("dve", lambda E: E.tensor_tensor(out=cf[:], in0=cf[:], in1=bc(rstd[:].unsqueeze(1), [128, 8, 512]), op=ALU.mult), reads=[Bcf, Brstd], writes=[Bcf])
            for c in range(8):
                P.op("act", lambda E, c=c: E.activation(yB[:, c, :], cf[:, c, :], AF.Silu, bias=V("lnb", c), scale=V("lnw", c)), reads=[Bcf, Bvec], writes=[By])
            P.dma("sp", t["yB"].rearrange("(c p) n -> p c n", p=128)[:, :, t0:t0 + 512], yB[:], reads=[By], writes=[ByB])
    P.barrier()


def build_mixer(cfg):
    nc = bass.Bass("TRN2", target_bir_lowering=False)
    D, S, NT, KD = cfg.D, cfg.S, cfg.NT, cfg.KD
    L = vecm_layout(KD)
    t = {}
    def I(name, shape, dt=F32):
        t[name] = nc.dram_tensor(name, shape, dt, kind="ExternalInput").ap()
    I("xTf", [D, S]); I("xTq", [D, HB + NT]); I("w_acd", [16, 128, KD, 128]); I("w_b", [16, 128, KD, 128]); I("w_ab", [128, KD, 4])
    I("poolw", [128, 2, 256]); I("vecm", [128, L["_n"]]); I("cst", [128, NCST])
    for name, shape in (("pACD", [2048, S]), ("pab", [4, S]), ("glu", [1024, HB + NT])):
        t[name] = nc.dram_tensor(name, shape, F32, kind="Internal").ap()
    t["yACD"] = nc.dram_tensor("yACD", [768, S], BF16, kind="ExternalOutput").ap()
    t["yB"] = nc.dram_tensor("yB", [1024, NT], BF16, kind="ExternalOutput").ap()
    P = Prog(nc)
    cx = Ctx(nc, P, cfg)
    cx.L = L
    cx.Bscr = {n: Buf(n) for n in ("pACD", "pab", "glu", "yACD", "yB")}
    cx.vec = nc.alloc_sbuf_tensor("vecm_sb", [128, L["_n"]], F32); cx.Bvec = Buf("vec")
    cx.cst = nc.alloc_sbuf_tensor("cst_sb", [128, NCST], F32); cx.Bcst = Buf("cst")
    P.dma("sp", cx.vec[:], t["vecm"], writes=[cx.Bvec])
    P.dma("sp", cx.cst[:], t["cst"], writes=[cx.Bcst])
    emit_mixer_gemm(cx, t)
    emit_mixer_acd(cx, t)
    emit_mixer_b(cx, t)
    P.finish()
    P.emit()
    return nc, P

import numpy as np

POOL_W = (2, 4, 8, 16)


def tile_w(W):
    K, N = W.shape
    return np.ascontiguousarray(W.reshape(K // 128, 128, N // 128, 128).transpose(2, 1, 0, 3))


def prep_mixer(inp, i, b, j, cfg, x_b):
    D, S, NT, KD = cfg.D, cfg.S, cfg.NT, cfg.KD
    L = vecm_layout(KD)
    W = inp["w_in"][i]
    ar = np.arange(256)
    cols = np.concatenate([0 + 256 * j + ar, 1024 + 256 * j + ar, 2048 + 256 * j + ar,
                           5120 + 256 * j + ar, 6144 + 256 * j + ar, 7168 + 256 * j + ar, 8192 + 256 * j + ar,
                           9232 + 256 * j + ar])
    m = {}
    m["xTf"] = np.ascontiguousarray(x_b.T)
    s0 = j * NT
    xq = np.zeros((HB + NT, D), np.float32)
    xq[HB:] = x_b[s0:s0 + NT]
    if j > 0:
        xq[:HB] = x_b[s0 - HB:s0]
    m["xTq"] = np.ascontiguousarray(xq.T)
    m["w_acd"] = tile_w(W[:, cols])
    m["w_b"] = tile_w(W[:, 3072:5120])
    ab = W[:, [9216 + 2 * j, 9217 + 2 * j, 9224 + 2 * j, 9225 + 2 * j]]
    m["w_ab"] = np.ascontiguousarray(ab.reshape(KD, 128, 4).transpose(1, 0, 2))
    m["poolw"] = np.ascontiguousarray(inp["pool_w"][i][j].reshape(2, 128, 256).transpose(1, 0, 2))
    v = np.zeros((128, L["_n"]), np.float32)
    v[:, L["nw1"]:L["nw1"] + KD] = pk(inp["attn_norm_w"][i])
    for c in range(2):
        for tap in range(3):
            v[:, L["scw"] + c * 3 + tap] = inp["sc_conv_w"][i][tap, 256 * j + c * 128:256 * j + (c + 1) * 128]
    for c in range(6):
        grp, h = c // 2, c % 2
        ch0 = grp * 1024 + 256 * j + h * 128
        for tap in range(4):
            v[:, L["gcw"] + c * 4 + tap] = inp["gdn_conv_w"][i][tap, ch0:ch0 + 128]
    for h in range(2):
        v[:, L["alog"] + h] = inp["gdn_a_log"][i][2 * j + h]
        v[:, L["dtb"] + h] = inp["gdn_dt_bias"][i][2 * j + h]
    v[:, L["gnw"]] = inp["gdn_norm_w"][i]
    for dm in range(2):
        v[:, L["pscale"] + dm] = inp["pool_scale"][i][256 * j + dm * 128:256 * j + (dm + 1) * 128]
    v[:, L["psel"] + j] = 1.0
    Wn = POOL_W[j]
    v[:, L["pinvw"]] = 1.0 / Wn
    v[:, L["pinvtab"]:L["pinvtab"] + 16] = 1.0 / np.minimum(np.arange(16) + 1, Wn)[None, :]
    for c in range(8):
        for tap in range(31):
            v[:, L["cfw"] + c * 31 + tap] = inp["cf_conv_w"][i][tap, c * 128:(c + 1) * 128]
    v[:, L["cfb"]:L["cfb"] + 8] = pk(inp["cf_conv_b"][i])
    v[:, L["lnw"]:L["lnw"] + 8] = pk(inp["cf_ln_w"][i])
    v[:, L["lnb"]:L["lnb"] + 8] = pk(inp["cf_ln_b"][i])
    m["vecm"] = v
    m["cst"] = make_cst()
    return m


_prog_cache = {}


def _get_prog(kind, cfg, last=False):
    key = (kind, cfg.D, cfg.S, cfg.FF, last)
    if key not in _prog_cache:
        _prog_cache[key] = build_mixer(cfg)[0] if kind == "M" else build_ffn(cfg, last)[0]
    return _prog_cache[key]


def kernel_unfused(inp, cfg, depth):
    x = np.asarray(inp["x"], np.float32)
    NT = cfg.NT
    cur = [np.array(x[b]) for b in range(2)]
    for i in range(depth):
        ncM = _get_prog("M", cfg)
        in_maps = [prep_mixer(inp, i, c // 4, c % 4, cfg, cur[c // 4]) for c in range(8)]
        res = run_bass_kernel_spmd(ncM, in_maps, core_ids=list(range(8))).results
        del in_maps
        last = (i == depth - 1)
        ncF = _get_prog("F", cfg, last)
        wout = tile_w(np.asarray(inp["w_out"][i], np.float32))
        wg = tile_w(np.asarray(inp["w_gate"][i], np.float32))
        wu = tile_w(np.asarray(inp["w_up"][i], np.float32))
        wd = tile_w(np.asarray(inp["w_down"][i], np.float32))
        vec = np.concatenate([pk(inp["ffn_norm_w"][i]), pk(inp["final_norm_w"])], axis=1)
        in_maps = []
        for c in range(8):
            b, j = c // 4, c % 4
            ts = slice(j * NT, (j + 1) * NT)
            mixT = np.empty((4096, NT), ml_dtypes.bfloat16)
            for jj in range(4):
                y = res[b * 4 + jj]["yACD"]
                mixT[0 + 256 * jj:0 + 256 * (jj + 1)] = y[0:256, ts]
                mixT[2048 + 256 * jj:2048 + 256 * (jj + 1)] = y[256:512, ts]
                mixT[3072 + 256 * jj:3072 + 256 * (jj + 1)] = y[512:768, ts]
            mixT[1024:2048] = res[c]["yB"]
            in_maps.append({"xT": np.ascontiguousarray(cur[b][ts].T), "mixT": mixT, "wout": wout, "wg": wg, "wu": wu, "wd": wd, "vec": vec})
        del res
        res = run_bass_kernel_spmd(ncF, in_maps, core_ids=list(range(8))).results
        del in_maps
        for c in range(8):
            b, j = c // 4, c % 4
            cur[b][j * NT:(j + 1) * NT] = res[c]["outT"].T
        del res
    return np.stack(cur).astype(np.float32)


def kernel(**inputs):
    B, S, D = inputs["x"].shape
    FF = inputs["w_gate"].shape[2]
    depth = inputs["w_in"].shape[0]
    cfg = Cfg(D=D, S=S, FF=FF)
    return kernel_unfused(inputs, cfg, depth)
```
